# Optimizing a Trainium2 kernel written in Bass

```python
import math
import jax, jax.numpy as jnp
from jax import lax
import numpy as np

D_MODEL = 4096
BATCH = 2
SEQ = 8192
DEPTH = 4

CHUNK = 64
N_MIXERS = 2
N_HEADS = 32
HEAD_DIM = D_MODEL // N_HEADS
N_KV_HEADS = 8
GROUP = N_HEADS // N_KV_HEADS
D_ATTN = N_HEADS * HEAD_DIM
N_IDX_HEADS = 32
IDX_HEAD_DIM = 64
TOPK_MAX = 256
Q_BLOCK = 128
ROPE_THETA = 10000.0
ATTN_SIZES = (D_ATTN, N_KV_HEADS * HEAD_DIM, N_KV_HEADS * HEAD_DIM, N_IDX_HEADS * IDX_HEAD_DIM, N_IDX_HEADS, IDX_HEAD_DIM)
ATTN_IN = sum(ATTN_SIZES)
ATTN_SPLIT_POINTS = tuple(int(v) for v in np.cumsum(ATTN_SIZES)[:-1])
HGRN_EXPAND = 128
HGRN_HEADS = D_MODEL // HGRN_EXPAND
HGRN_KEY_DIM = HGRN_HEADS * HGRN_EXPAND
HGRN_HEAD_V = D_MODEL // HGRN_HEADS
D_FF = 4 * D_MODEL
ALPHA = (2.0 * DEPTH) ** 0.25
BETA = (8.0 * DEPTH) ** -0.25
LN_EPS = 1e-5
RMS_EPS = 1e-6
N_ATTN_LAYERS = (DEPTH + 1) // 2
N_HGRN_LAYERS = DEPTH // 2

kernel_name = "dsa_hgrn2_deepnorm_hybrid"


def layer_norm(x, g, b):
    xf = x.astype(jnp.float32)
    mu = jnp.mean(xf, axis=-1, keepdims=True)
    var = jnp.mean(jnp.square(xf - mu), axis=-1, keepdims=True)
    y = (xf - mu) * lax.rsqrt(var + LN_EPS) * g.astype(jnp.float32) + b.astype(jnp.float32)
    return y.astype(x.dtype)


def rope_tables(seq, dim):
    inv = 1.0 / (ROPE_THETA ** (jnp.arange(0, dim, 2, dtype=jnp.float32) / dim))
    ang = jnp.arange(seq, dtype=jnp.float32)[:, None] * inv[None, :]
    return jnp.cos(ang), jnp.sin(ang)


def apply_rope(x, cos, sin):
    half = x.shape[-1] // 2
    x1 = x[..., :half].astype(jnp.float32)
    x2 = x[..., half:].astype(jnp.float32)
    return jnp.concatenate([x1 * cos - x2 * sin, x2 * cos + x1 * sin], axis=-1).astype(x.dtype)


def dsa_mixer(x, w_in, w_out, kln_g, kln_b):
    B, S, _ = x.shape
    topk = min(TOPK_MAX, S // 4)
    z = x @ w_in
    q, k, v, qi, wi, ki = jnp.split(z, ATTN_SPLIT_POINTS, axis=-1)
    cos, sin = rope_tables(S, HEAD_DIM)
    q = apply_rope(q.reshape(B, S, N_HEADS, HEAD_DIM), cos[:, None], sin[:, None])
    k = apply_rope(k.reshape(B, S, N_KV_HEADS, HEAD_DIM), cos[:, None], sin[:, None])
    v = v.reshape(B, S, N_KV_HEADS, HEAD_DIM)
    ci, si = rope_tables(S, IDX_HEAD_DIM)
    qi = apply_rope(qi.reshape(B, S, N_IDX_HEADS, IDX_HEAD_DIM), ci[:, None], si[:, None])
    ki = apply_rope(layer_norm(ki, kln_g, kln_b), ci, si)
    wi = wi.astype(jnp.float32) * (N_IDX_HEADS ** -0.5)
    key_pos = jnp.arange(S)

    def block(blk):
        start = blk * Q_BLOCK
        q_pos = start + jnp.arange(Q_BLOCK)
        chunk_end = (q_pos // CHUNK + 1) * CHUNK - 1
        qi_b = lax.dynamic_slice_in_dim(qi, start, Q_BLOCK, axis=1)
        wi_b = lax.dynamic_slice_in_dim(wi, start, Q_BLOCK, axis=1)
        q_b = lax.dynamic_slice_in_dim(q, start, Q_BLOCK, axis=1)
        logits = jnp.einsum('bthd,bsd->bths', qi_b, ki, preferred_element_type=jnp.float32) * (IDX_HEAD_DIM ** -0.5)
        score = jnp.einsum('bths,bth->bts', jax.nn.relu(logits), wi_b)
        admissible = key_pos[None, :] <= chunk_end[:, None]
        score = jnp.where(admissible[None], score, -jnp.inf)
        _, sel = lax.top_k(score, topk)
        valid = sel <= chunk_end[None, :, None]
        k_sel = jax.vmap(lambda kk, ii: kk[ii])(k, sel)
        v_sel = jax.vmap(lambda vv, ii: vv[ii])(v, sel)
        qg = q_b.reshape(B, Q_BLOCK, N_KV_HEADS, GROUP, HEAD_DIM)
        s = jnp.einsum('btgrd,btkgd->btgrk', qg, k_sel, preferred_element_type=jnp.float32) * (HEAD_DIM ** -0.5)
        s = jnp.where(valid[:, :, None, None, :], s, -jnp.inf)
        p = jax.nn.softmax(s, axis=-1).astype(v.dtype)
        o = jnp.einsum('btgrk,btkgd->btgrd', p, v_sel)
        return o.reshape(B, Q_BLOCK, D_ATTN)

    out = lax.map(block, jnp.arange(S // Q_BLOCK))
    out = jnp.moveaxis(out, 0, 1).reshape(B, S, D_ATTN)
    return out @ w_out


def hgrn2_mixer(x, w_in, lb, norm_g, w_out):
    B, S, _ = x.shape
    n_chunks = S // CHUNK
    q, f, i, g = jnp.split(x @ w_in, 4, axis=-1)
    q = jax.nn.silu(q.astype(jnp.float32))
    f = lb + (1.0 - lb) * jax.nn.sigmoid(f.astype(jnp.float32))
    log_f = jnp.log(f)
    kin = 1.0 - f
    v = i.astype(jnp.float32)

    def to_chunks(t, d):
        return t.reshape(B, n_chunks, CHUNK, HGRN_HEADS, d).transpose(1, 0, 3, 2, 4)

    qc = to_chunks(q, HGRN_EXPAND)
    kc = to_chunks(kin, HGRN_EXPAND)
    gc = to_chunks(log_f, HGRN_EXPAND)
    vc = to_chunks(v, HGRN_HEAD_V)
    causal = jnp.tril(jnp.ones((CHUNK, CHUNK), dtype=bool))

    def step(state, inp):
        qt, kt, gt, vt = inp
        bcum = jnp.cumsum(gt, axis=2)
        o_inter = jnp.einsum('bhtk,bhkv->bhtv', qt * jnp.exp(bcum), state)
        diff = bcum[:, :, :, None, :] - bcum[:, :, None, :, :]
        decay = jnp.exp(jnp.where(causal[:, :, None], diff, -jnp.inf))
        attn = jnp.einsum('bhtk,bhsk,bhtsk->bhts', qt, kt, decay)
        o = o_inter + jnp.einsum('bhts,bhsv->bhtv', attn, vt)
        b_last = bcum[:, :, -1:, :]
        state = jnp.exp(b_last[:, :, 0, :])[..., None] * state + jnp.einsum('bhsk,bhsv->bhkv', kt * jnp.exp(b_last - bcum), vt)
        return state, o

    state0 = jnp.zeros((B, HGRN_HEADS, HGRN_EXPAND, HGRN_HEAD_V), jnp.float32)
    _, o = lax.scan(step, state0, (qc, kc, gc, vc))
    o = o.transpose(1, 0, 3, 2, 4).reshape(B, S, HGRN_HEADS, HGRN_HEAD_V)
    o = o * lax.rsqrt(jnp.mean(jnp.square(o), axis=-1, keepdims=True) + RMS_EPS) * norm_g.astype(jnp.float32)
    o = (o.reshape(B, S, D_MODEL) * jax.nn.silu(g.astype(jnp.float32))).astype(x.dtype)
    return o @ w_out


def sq_relu_mlp(x, w_up, w_down):
    h = jnp.square(jax.nn.relu(x @ w_up))
    return h @ w_down


def setup_inputs(seed: int = 0) -> dict:
    key = jax.random.key(seed)
    ks = jax.random.split(key, 16)
    f32 = jnp.float32
    nrm = lambda k, shape, scale: jax.random.normal(k, shape, f32) * scale
    return {
        "x": nrm(ks[0], (BATCH, SEQ, D_MODEL), 1.0),
        "attn_w_in": nrm(ks[1], (N_ATTN_LAYERS, D_MODEL, ATTN_IN), D_MODEL ** -0.5),
        "attn_w_out": nrm(ks[2], (N_ATTN_LAYERS, D_ATTN, D_MODEL), BETA * D_ATTN ** -0.5),
        "idx_k_ln_g": 1.0 + nrm(ks[3], (N_ATTN_LAYERS, IDX_HEAD_DIM), 0.02),
        "idx_k_ln_b": nrm(ks[4], (N_ATTN_LAYERS, IDX_HEAD_DIM), 0.02),
        "hgrn_w_in": nrm(ks[5], (N_HGRN_LAYERS, D_MODEL, 4 * D_MODEL), D_MODEL ** -0.5),
        "hgrn_lower_bounds": nrm(ks[6], (DEPTH, HGRN_KEY_DIM), 0.1),
        "hgrn_norm_g": 1.0 + nrm(ks[7], (N_HGRN_LAYERS, HGRN_HEAD_V), 0.02),
        "hgrn_w_out": nrm(ks[8], (N_HGRN_LAYERS, D_MODEL, D_MODEL), BETA * D_MODEL ** -0.5),
        "mix_ln_g": 1.0 + nrm(ks[9], (DEPTH, D_MODEL), 0.02),
        "mix_ln_b": nrm(ks[10], (DEPTH, D_MODEL), 0.02),
        "mlp_w_up": nrm(ks[11], (DEPTH, D_MODEL, D_FF), D_MODEL ** -0.5),
        "mlp_w_down": nrm(ks[12], (DEPTH, D_FF, D_MODEL), BETA * D_FF ** -0.5),
        "mlp_ln_g": 1.0 + nrm(ks[13], (DEPTH, D_MODEL), 0.02),
        "mlp_ln_b": nrm(ks[14], (DEPTH, D_MODEL), 0.02),
    }


def reference(x, attn_w_in, attn_w_out, idx_k_ln_g, idx_k_ln_b, hgrn_w_in, hgrn_lower_bounds, hgrn_norm_g, hgrn_w_out, mix_ln_g, mix_ln_b, mlp_w_up, mlp_w_down, mlp_ln_g, mlp_ln_b):
    lb_all = jnp.cumsum(jax.nn.softmax(hgrn_lower_bounds.astype(jnp.float32), axis=0), axis=0)
    lb_all = lb_all - lb_all[0:1]
    for layer in range(DEPTH):
        j = layer // N_MIXERS
        if layer % N_MIXERS == 0:
            h = dsa_mixer(x, attn_w_in[j], attn_w_out[j], idx_k_ln_g[j], idx_k_ln_b[j])
        else:
            h = hgrn2_mixer(x, hgrn_w_in[j], lb_all[layer], hgrn_norm_g[j], hgrn_w_out[j])
        x = layer_norm(ALPHA * x + h, mix_ln_g[layer], mix_ln_b[layer])
        h = sq_relu_mlp(x, mlp_w_up[layer], mlp_w_down[layer])
        x = layer_norm(ALPHA * x + h, mlp_ln_g[layer], mlp_ln_b[layer])
    return x
```

```python
import math
from contextlib import ExitStack
import numpy as np
import ml_dtypes
import concourse.bass as bass
import concourse.mybir as mybir
from concourse.bass_utils import run_bass_kernel_spmd

F32 = mybir.dt.float32
BF16 = mybir.dt.bfloat16
AF = mybir.ActivationFunctionType
ALU = mybir.AluOpType
AX = mybir.AxisListType

D = 4096
KC = D // 128
DFF = 4 * D
DEPTH = 4
ALPHA = (2.0 * DEPTH) ** 0.25
LN_EPS = 1e-5
RMS_EPS = 1e-6
ATTN_IN = 8288
NEG = -1.0e30


class Buf:
    __slots__ = ("w", "r", "rp", "multi")

    def __init__(self, multi=False):
        self.w = {}
        self.r = {}
        self.rp = {}
        self.multi = multi


class Sync:
    def __init__(self, nc, stack, n_dma_sems=8):
        self.nc = nc
        self.eng = {"pe": nc.tensor, "act": nc.scalar, "dve": nc.vector, "pool": nc.gpsimd, "sp": nc.sync}
        self.sem = {}
        self.cnt = {}
        self.seen = {k: {} for k in self.eng}
        for k in self.eng:
            self.sem[k] = stack.enter_context(nc.semaphore("s_" + k))
            self.cnt[k] = 0
        self.dsem = {}
        self.dnext = {}
        for q in ("sp", "act", "pool"):
            self.dsem[q] = []
            for i in range(n_dma_sems):
                key = "d_%s%d" % (q, i)
                self.sem[key] = stack.enter_context(nc.semaphore(key))
                self.cnt[key] = 0
                self.dsem[q].append(key)
            self.dnext[q] = 0
        self.n_inst = 0

    def _need(self, e, deps):
        eng = self.eng[e]
        seen = self.seen[e]
        for k, v in deps.items():
            if seen.get(k, 0) >= v:
                continue
            eng.wait_ge(self.sem[k], v)
            seen[k] = v

    @staticmethod
    def _acc(deps, d, skip=None):
        for k, v in d.items():
            if k == skip:
                continue
            if deps.get(k, 0) < v:
                deps[k] = v

    def _deps(self, e, reads, writes):
        deps = {}
        for b in reads:
            self._acc(deps, b.w, skip=("pe" if e == "pe" else None))
        for b in writes:
            if not b.multi:
                self._acc(deps, b.w, skip=e)
            self._acc(deps, b.r, skip=e)
            self._acc(deps, b.rp, skip=e)
        return deps

    def _update(self, key, v, reads, writes):
        for b in writes:
            if b.r:
                b.rp = b.r
                b.r = {}
                b.w = {key: v}
            elif b.multi:
                b.w[key] = v
            else:
                b.w = {key: v}
        for b in reads:
            b.r[key] = v

    def op(self, e, fn, reads=(), writes=()):
        self._need(e, self._deps(e, reads, writes))
        inst = fn(self.eng[e])
        self.cnt[e] += 1
        inst.then_inc(self.sem[e], 1)
        self._update(e, self.cnt[e], reads, writes)
        self.n_inst += 1
        return inst

    def dma(self, q, out, in_, reads=(), writes=(), **kw):
        key = self.dsem[q][self.dnext[q] % len(self.dsem[q])]
        self.dnext[q] += 1
        deps = self._deps("dma", reads, writes)
        if self.cnt[key] > 0 and deps.get(key, 0) < self.cnt[key]:
            deps[key] = self.cnt[key]
        self._need(q, deps)
        inst = self.eng[q].dma_start(out=out, in_=in_, **kw)
        self.cnt[key] += 16
        inst.then_inc(self.sem[key], 16)
        self._update(key, self.cnt[key], reads, writes)
        self.n_inst += 1
        return inst

    def barrier(self):
        for e in self.eng:
            deps = {k: v for k, v in self.cnt.items() if v > 0 and k != e}
            self._need(e, deps)


def bcast_rows(ap1d, n, parts=128):
    return bass.AP(ap1d.tensor, ap1d.offset, [[0, parts], [1, n]])


class Prog:
    def __init__(self, T, layers=(0, 1, 2, 3), small=False, do_mixer=True, do_mlp=True):
        self.T = T
        self.layers = list(layers)
        self.small = small
        self.do_mixer = do_mixer
        self.do_mlp = do_mlp
        self.attn_upto = 4
        self.nc = bass.Bass("TRN2", target_bir_lowering=False)
        self.root = ExitStack()
        self.S = Sync(self.nc, self.root)
        self.wb = {}
        self.bufs = {}

    def din(self, name, shape, dt=F32):
        return self.nc.dram_tensor(name, list(shape), dt, kind="ExternalInput").ap()

    def dout(self, name, shape, dt=F32):
        return self.nc.dram_tensor(name, list(shape), dt, kind="ExternalOutput").ap()

    def dscr(self, name, shape, dt):
        self.bufs[name] = Buf(multi=True)
        return self.nc.dram_tensor(name, list(shape), dt).ap()

    def sb(self, st, name, shape, dt):
        self.uid = getattr(self, "uid", 0) + 1
        return st.enter_context(self.nc.sbuf_tensor("%s_u%d" % (name, self.uid), list(shape), dt))

    def ps(self, st, name, shape, dt):
        self.uid = getattr(self, "uid", 0) + 1
        return st.enter_context(self.nc.psum_tensor("%s_u%d" % (name, self.uid), list(shape), dt))

    def convert_weight(self, name, W, K, col0, ncols):
        nc, S = self.nc, self.S
        nkg = K // 2048
        npan = (ncols + 511) // 512
        wb = self.dscr("wb_" + name, [nkg, npan, 128, 16 * 512], BF16)
        self.wb[name] = (wb, nkg, npan, ncols)
        wbuf = self.bufs["wb_" + name]
        Wv = W.rearrange("(c p) n -> p c n", p=128)
        for kg in range(nkg):
            for pn in range(npan):
                w = min(512, ncols - pn * 512)
                for hf in range(2):
                    i = self.cv_i
                    self.cv_i += 1
                    f = self.cv_f[i % 3]
                    b = self.cv_b[i % 3]
                    bf_, bb_ = self.cv_fb[i % 3], self.cv_bb[i % 3]
                    c0 = kg * 16 + hf * 8
                    S.dma("sp" if i % 2 == 0 else "act", f[:, :, 0:w], Wv[:, c0:c0 + 8, col0 + pn * 512: col0 + pn * 512 + w], writes=[bf_])
                    eng = ("dve", "pool", "act")[i % 3]
                    if eng == "act":
                        S.op("act", lambda e: e.copy(b[:, :, 0:w], f[:, :, 0:w]), reads=[bf_], writes=[bb_])
                    else:
                        S.op(eng, lambda e: e.tensor_copy(b[:, :, 0:w], f[:, :, 0:w]), reads=[bf_], writes=[bb_])
                    dst = wb[kg, pn].rearrange("p (c n) -> p c n", n=512)
                    S.dma("sp", dst[:, hf * 8:(hf + 1) * 8, 0:w], b[:, :, 0:w], reads=[bb_], writes=[wbuf])

    def gemm_F(self, st, aT, abuf, wname, panels, epi, tag):
        nc, S, T = self.nc, self.S, self.T
        TB = min(1024, T)
        NH = TB // 512
        wb, nkg, npan, ncols = self.wb[wname]
        wbuf = self.bufs["wb_" + wname]
        at = self.sb(st, tag + "_at", [128, KC, TB], BF16)
        b_at = [Buf(multi=True) for _ in range(NH)]
        wt = [self.sb(st, tag + "_wt%d" % i, [128, 16, 512], BF16) for i in range(4)]
        b_wt = [Buf() for _ in range(4)]
        aTv = aT.rearrange("(c p) t -> p c t", p=128)
        wi = 0
        for tb in range(T // TB):
            for th in range(NH):
                t0 = tb * TB + th * 512
                for cg in range(2):
                    S.dma("sp" if cg == 0 else "act", at[:, cg * 16:(cg + 1) * 16, th * 512:(th + 1) * 512],
                          aTv[:, cg * 16:(cg + 1) * 16, t0:t0 + 512], reads=[abuf], writes=[b_at[th]])
            for pn in panels:
                slots = []
                for kg in range(2):
                    s = wi % 4
                    wi += 1
                    slots.append(s)
                    S.dma("sp" if kg == 0 else "act", wt[s][:, :, :], wb[kg, pn].rearrange("p (c n) -> p c n", n=512),
                          reads=[wbuf], writes=[b_wt[s]])
                w = min(512, ncols - pn * 512)
                for j in range((w + 127) // 128):
                    m = min(128, w - j * 128)
                    for th in range(NH):
                        p = self.ps_i % 4
                        self.ps_i += 1
                        pst, psb = self.psg[p], self.b_psg[p]
                        for c in range(KC):
                            s = slots[c // 16]
                            S.op("pe", lambda e: e.matmul(pst[0:m, :], wt[s][:, c % 16, j * 128:j * 128 + m], at[:, c, th * 512:(th + 1) * 512],
                                                          start=(c == 0), stop=(c == KC - 1)),
                                 reads=[b_wt[s], b_at[th]], writes=[psb])
                        epi(tb * TB + th * 512, pn, j, m, pst, psb)

    def gemm_T(self, st, aT, abuf, kc_total, wname, panels, epi, fin, tag):
        nc, S, T = self.nc, self.S, self.T
        TB = 256
        wb, nkg, npan, ncols = self.wb[wname]
        wbuf = self.bufs["wb_" + wname]
        assert nkg == kc_total // 16
        at = self.sb(st, tag + "_at", [128, kc_total, TB], BF16)
        b_at = Buf(multi=True)
        wt = [self.sb(st, tag + "_wt%d" % i, [128, 16, 512], BF16) for i in range(3)]
        b_wt = [Buf() for _ in range(3)]
        aTv = aT.rearrange("(c p) t -> p c t", p=128)
        wi = 0
        for tb in range(T // TB):
            t0 = tb * TB
            ng = kc_total // 16
            for cg in range(ng):
                S.dma("sp" if cg % 2 == 0 else "act", at[:, cg * 16:(cg + 1) * 16, :], aTv[:, cg * 16:(cg + 1) * 16, t0:t0 + TB],
                      reads=[abuf], writes=[b_at])
            for pn in panels:
                w = min(512, ncols - pn * 512)
                p0 = self.ps_i % 4
                self.ps_i += 2
                pss = [(self.psg[(p0 + i) % 4], self.b_psg[(p0 + i) % 4]) for i in range(2)]
                for kg in range(nkg):
                    s = wi % 3
                    wi += 1
                    S.dma("sp" if wi % 2 == 0 else "act", wt[s][:, :, 0:w], wb[kg, pn].rearrange("p (c n) -> p c n", n=512)[:, :, 0:w],
                          reads=[wbuf], writes=[b_wt[s]])
                    for ts in range(2):
                        pst, psb = pss[ts]
                        for c in range(16):
                            cc = kg * 16 + c
                            S.op("pe", lambda e: e.matmul(pst[:, 0:w], at[:, cc, ts * 128:(ts + 1) * 128], wt[s][:, c, 0:w],
                                                          start=(cc == 0), stop=(cc == kc_total - 1)),
                                 reads=[b_wt[s], b_at], writes=[psb])
                for ts in range(2):
                    epi(t0 + ts * 128, ts, pn, w, pss[ts][0], pss[ts][1])
            if fin is not None:
                fin(t0)

    def emit_xT(self, xb, b_xb, xT_dram, xT_buf, t0):
        S = self.S
        k = self.xt_i % 2
        self.xt_i += 1
        xts, b_xts = self.xts[k], self.b_xts[k]
        for g in range(8):
            q = self.pt_i % 2
            self.pt_i += 1
            pt, b_pt = self.pst[q], self.b_pst[q]
            for i in range(4):
                c = g * 4 + i
                S.op("pe", lambda e: e.transpose(pt[:, i * 128:(i + 1) * 128], xb[:, c * 128:(c + 1) * 128], self.ident[:, :]),
                     reads=[b_xb, self.b_const], writes=[b_pt])
            eng = "act" if g % 2 == 0 else "dve"
            dst = xts[:, g * 4:(g + 1) * 4, :]
            src = pt[:, 0:512].rearrange("p (c t) -> p c t", t=128)
            if eng == "act":
                S.op("act", lambda e: e.copy(dst, src), reads=[b_pt], writes=[b_xts])
            else:
                S.op("dve", lambda e: e.tensor_copy(dst, src), reads=[b_pt], writes=[b_xts])
        S.dma("pool", xT_dram.rearrange("(c p) t -> p c t", p=128)[:, :, t0:t0 + 128], xts[:, :, :], reads=[b_xts], writes=[xT_buf])

    def emit_ln(self, r, b_r, g_bc, b_bc, x_out, x_out_buf, xT_dram, xT_buf, t0):
        S, nc = self.S, self.nc
        k = self.ln_i % 2
        self.ln_i += 1
        st6, b_st = self.ln_stats[k], self.b_ln_stats[k]
        mv, b_mv = self.ln_mv[k], self.b_ln_mv[k]
        xb, b_xb = self.ln_xb[k], self.b_ln_xb[k]
        for i in range(8):
            S.op("dve", lambda e: e.bn_stats(st6[:, i * 6:(i + 1) * 6], r[:, i * 512:(i + 1) * 512]), reads=[b_r], writes=[b_st])
        S.op("dve", lambda e: e.bn_aggr(mv[:, 0:2], st6[:, :]), reads=[b_st], writes=[b_mv])
        S.op("dve", lambda e: e.tensor_scalar_add(mv[:, 2:3], mv[:, 1:2], LN_EPS), reads=[b_mv], writes=[b_mv])
        S.op("act", lambda e: e.sqrt(mv[:, 2:3], mv[:, 2:3]), reads=[b_mv], writes=[b_mv])
        S.op("dve", lambda e: e.reciprocal(mv[:, 2:3], mv[:, 2:3]), reads=[b_mv], writes=[b_mv])
        S.op("dve", lambda e: e.scalar_tensor_tensor(mv[:, 3:4], mv[:, 0:1], -1.0, mv[:, 2:3], op0=ALU.mult, op1=ALU.mult), reads=[b_mv], writes=[b_mv])
        S.op("act", lambda e: e.activation(r[:, :], r[:, :], AF.Identity, bias=mv[:, 3:4], scale=mv[:, 2:3]), reads=[b_r, b_mv], writes=[b_r])
        S.op("dve", lambda e: e.tensor_tensor(r[:, :], r[:, :], g_bc[:, :], ALU.mult), reads=[b_r, self.b_gb], writes=[b_r])
        S.op("pool", lambda e: e.tensor_tensor(r[:, :], r[:, :], b_bc[:, :], ALU.add), reads=[b_r, self.b_gb], writes=[b_r])
        S.dma("sp", x_out[t0:t0 + 128, :], r[:, :], reads=[b_r], writes=[x_out_buf])
        S.op("act", lambda e: e.copy(xb[:, :], r[:, :]), reads=[b_r], writes=[b_xb])
        if xT_dram is not None:
            self.emit_xT(xb, b_xb, xT_dram, xT_buf, t0)

    def setup_common(self):
        nc, st = self.nc, self.root
        self.psg = [self.ps(st, "psg%d" % i, [128, 512], F32) for i in range(4)]
        self.b_psg = [Buf() for _ in range(4)]
        self.pst = [self.ps(st, "pst%d" % i, [128, 1024], BF16) for i in range(2)]
        self.b_pst = [Buf() for _ in range(2)]
        self.ps_i = 0
        self.pt_i = 0
        self.xt_i = 0
        self.ln_i = 0
        self.ident = self.sb(st, "ident", [128, 128], BF16)
        self.b_const = Buf(multi=True)
        self.S.dma("sp", self.ident[:, :], self.c_ident[:, :], writes=[self.b_const])

    def alloc_ln(self, st):
        self.xts = [self.sb(st, "xts%d" % i, [128, KC, 128], BF16) for i in range(1)] * 2
        self.b_xts = [Buf()] * 2
        self.ln_stats = [self.sb(st, "lnst%d" % i, [128, 48], F32) for i in range(2)]
        self.b_ln_stats = [Buf() for _ in range(2)]
        self.ln_mv = [self.sb(st, "lnmv%d" % i, [128, 4], F32) for i in range(2)]
        self.b_ln_mv = [Buf() for _ in range(2)]
        self.ln_xb = [self.sb(st, "lnxb%d" % i, [128, D], BF16) for i in range(1)] * 2
        self.b_ln_xb = [Buf()] * 2

    def stage_out_ln(self, aT, abuf, kc_total, wname, g_vec, b_vec, x_out, x_out_buf, tag, last=False):
        S, nc, T = self.S, self.nc, self.T
        with ExitStack() as st:
            self.alloc_ln(st)
            g_bc = self.sb(st, tag + "_g", [128, D], F32)
            b_bc = self.sb(st, tag + "_b", [128, D], F32)
            self.b_gb = Buf(multi=True)
            S.dma("sp", g_bc[:, :], bcast_rows(g_vec, D), writes=[self.b_gb])
            S.dma("act", b_bc[:, :], bcast_rows(b_vec, D), writes=[self.b_gb])
            r = [self.sb(st, tag + "_r%d" % i, [128, D], F32) for i in range(2)]
            b_r = [Buf(multi=True) for _ in range(2)]
            xt_buf = self.bufs["xT"]
            xtok_buf = self.bufs["x_tok"]

            def epi(t0, ts, pn, w, pst, psb):
                if pn == 0:
                    S.dma("pool", r[ts][:, :], self.x_tok[t0:t0 + 128, :], reads=[xtok_buf], writes=[b_r[ts]])
                sl = slice(pn * 512, pn * 512 + w)
                S.op("dve", lambda e: e.scalar_tensor_tensor(r[ts][:, sl], r[ts][:, sl], ALPHA, pst[:, 0:w], op0=ALU.mult, op1=ALU.add),
                     reads=[b_r[ts], psb], writes=[b_r[ts]])

            def fin(t0):
                for ts in range(2):
                    self.emit_ln(r[ts], b_r[ts], g_bc, b_bc, x_out, x_out_buf, None if last else self.xT, xt_buf, t0 + ts * 128)

            self.gemm_T(st, aT, abuf, kc_total, wname, list(range(8)), epi, fin, tag)
        S.barrier()

    def stage_mlp_up(self, wname, tag):
        S = self.S
        with ExitStack() as st:
            ho = [self.sb(st, tag + "_ho%d" % i, [128, 512], BF16) for i in range(3)]
            tmp = [self.sb(st, tag + "_tmp%d" % i, [128, 512], F32) for i in range(3)]
            b_ho = [Buf() for _ in range(3)]
            b_tmp = [Buf() for _ in range(3)]
            cnt = [0]
            hbuf = self.bufs["hT"]

            def epi(t0, pn, j, m, pst, psb):
                i = cnt[0] % 3
                cnt[0] += 1
                S.op("act", lambda e: e.activation(tmp[i][:, :], pst[:, :], AF.Relu), reads=[psb], writes=[b_tmp[i]])
                S.op("dve" if cnt[0] % 2 else "pool", lambda e: e.tensor_tensor(ho[i][:, :], tmp[i][:, :], tmp[i][:, :], ALU.mult), reads=[b_tmp[i]], writes=[b_ho[i]])
                n0 = pn * 512 + j * 128
                S.dma("pool", self.hT[n0:n0 + 128, t0:t0 + 512], ho[i][:, :], reads=[b_ho[i]], writes=[hbuf])

            self.gemm_F(st, self.xT, self.bufs["xT"], wname, list(range(DFF // 512)), epi, tag)
        S.barrier()

    def stage_load_x(self, x_in):
        S, T = self.S, self.T
        with ExitStack() as st:
            self.alloc_ln(st)
            xf = [self.sb(st, "lx_f%d" % i, [128, D], F32) for i in range(2)]
            b_xf = [Buf() for _ in range(2)]
            for tt in range(T // 128):
                k = tt % 2
                S.dma("sp", xf[k][:, :], x_in[tt * 128:(tt + 1) * 128, :], writes=[b_xf[k]])
                S.dma("act", self.x_tok[tt * 128:(tt + 1) * 128, :], xf[k][:, :], reads=[b_xf[k]], writes=[self.bufs["x_tok"]])
                xb, b_xb = self.ln_xb[k], self.b_ln_xb[k]
                S.op("dve", lambda e: e.tensor_copy(xb[:, :], xf[k][:, :]), reads=[b_xf[k]], writes=[b_xb])
                self.emit_xT(xb, b_xb, self.xT, self.bufs["xT"], tt * 128)
        S.barrier()


    def setup_lb(self, lower_bounds):
        S, st = self.S, self.root
        e = self.sb(st, "lb_e", [128, DEPTH, KC], F32)
        ssum = self.sb(st, "lb_s", [128, KC], F32)
        acc = self.sb(st, "lb_acc", [128, KC], F32)
        self.lbt = self.sb(st, "lb_t", [128, DEPTH, 3, KC], F32)
        b = Buf()
        self.b_lb = b
        S.dma("sp", e[:, :, :], lower_bounds.rearrange("l (c p) -> p l c", p=128), writes=[b])
        S.op("act", lambda en: en.activation(e[:, :, :], e[:, :, :], AF.Exp), reads=[b], writes=[b])
        S.op("dve", lambda en: en.tensor_tensor(ssum[:, :], e[:, 0, :], e[:, 1, :], ALU.add), reads=[b], writes=[b])
        for l in range(2, DEPTH):
            S.op("dve", lambda en: en.tensor_tensor(ssum[:, :], ssum[:, :], e[:, l, :], ALU.add), reads=[b], writes=[b])
        S.op("dve", lambda en: en.reciprocal(ssum[:, :], ssum[:, :]), reads=[b], writes=[b])
        S.op("dve", lambda en: en.memset(acc[:, :], 0.0), reads=[b], writes=[b])
        for l in range(1, DEPTH):
            S.op("dve", lambda en: en.tensor_tensor(acc[:, :], acc[:, :], e[:, l, :], ALU.add), reads=[b], writes=[b])
            S.op("dve", lambda en: en.tensor_tensor(self.lbt[:, l, 0, :], acc[:, :], ssum[:, :], ALU.mult), reads=[b], writes=[b])
            S.op("dve", lambda en: en.tensor_scalar(self.lbt[:, l, 1, :], self.lbt[:, l, 0, :], -1.0, 1.0, op0=ALU.mult, op1=ALU.add), reads=[b], writes=[b])
            S.op("dve", lambda en: en.tensor_scalar(self.lbt[:, l, 2, :], self.lbt[:, l, 0, :], 1.0, -1.0, op0=ALU.mult, op1=ALU.add), reads=[b], writes=[b])

    def stage_hgrn_proj(self, layer, wname):
        S, T = self.S, self.T
        with ExitStack() as st:
            tq = [self.sb(st, "hp_q%d" % i, [128, 512], F32) for i in range(3)]
            tl = [self.sb(st, "hp_l%d" % i, [128, 512], F32) for i in range(3)]
            tk = [self.sb(st, "hp_k%d" % i, [128, 512], F32) for i in range(3)]
            bq = [Buf() for _ in range(3)]
            bl = [Buf() for _ in range(3)]
            bk = [Buf() for _ in range(3)]
            cnt = [0]
            lbt = self.lbt

            def epi(t0, pn, j, m, pst, psb):
                i = cnt[0] % 3
                cnt[0] += 1
                if pn < 8:
                    n0 = pn * 512 + j * 128
                    S.op("act", lambda e: e.activation(tq[i][:, :], pst[:, :], AF.Silu), reads=[psb], writes=[bq[i]])
                    S.dma("pool", self.qsT[n0:n0 + 128, t0:t0 + 512], tq[i][:, :], reads=[bq[i]], writes=[self.bufs["qsT"]])
                else:
                    c = (pn - 8) * 4 + j
                    n0 = c * 128
                    S.op("act", lambda e: e.activation(tq[i][:, :], pst[:, :], AF.Sigmoid), reads=[psb], writes=[bq[i]])
                    S.op("act", lambda e: e.activation(tl[i][:, :], tq[i][:, :], AF.Ln, bias=lbt[:, layer, 0, c:c + 1], scale=lbt[:, layer, 1, c:c + 1]),
                         reads=[bq[i], self.b_lb], writes=[bl[i]])
                    S.op("dve", lambda e: e.tensor_scalar(tk[i][:, :], tq[i][:, :], lbt[:, layer, 2, c:c + 1], lbt[:, layer, 1, c:c + 1], op0=ALU.mult, op1=ALU.add),
                         reads=[bq[i], self.b_lb], writes=[bk[i]])
                    S.dma("pool", self.logfT[n0:n0 + 128, t0:t0 + 512], tl[i][:, :], reads=[bl[i]], writes=[self.bufs["logfT"]])
                    S.dma("sp", self.kinT[n0:n0 + 128, t0:t0 + 512], tk[i][:, :], reads=[bk[i]], writes=[self.bufs["kinT"]])

            self.gemm_F(st, self.xT, self.bufs["xT"], wname, list(range(16)), epi, "hpF")
        S.barrier()
        with ExitStack() as st:
            tv = [self.sb(st, "hp_v%d" % i, [128, 512], BF16) for i in range(3)]
            tg = [self.sb(st, "hp_g%d" % i, [128, 512], F32) for i in range(3)]
            bv = [Buf() for _ in range(3)]
            bg = [Buf() for _ in range(3)]
            cnt = [0]

            def epi(t0, ts, pn, w, pst, psb):
                i = cnt[0] % 3
                cnt[0] += 1
                if pn < 24:
                    n0 = (pn - 16) * 512
                    S.op("act", lambda e: e.copy(tv[i][:, :], pst[:, :]), reads=[psb], writes=[bv[i]])
                    S.dma("pool", self.v_tok[t0:t0 + 128, n0:n0 + 512], tv[i][:, :], reads=[bv[i]], writes=[self.bufs["v_tok"]])
                else:
                    n0 = (pn - 24) * 512
                    S.op("act", lambda e: e.activation(tg[i][:, :], pst[:, :], AF.Silu), reads=[psb], writes=[bg[i]])
                    S.dma("pool", self.sg_tok[t0:t0 + 128, n0:n0 + 512], tg[i][:, :], reads=[bg[i]], writes=[self.bufs["sg_tok"]])

            self.gemm_T(st, self.xT, self.bufs["xT"], KC, wname, list(range(16, 32)), epi, None, "hpT")
        S.barrier()

    def stage_hgrn_core(self, norm_g):
        S, T, nc = self.S, self.T, self.nc
        NB = T // 512
        with ExitStack() as st:
            rmask = self.sb(st, "hc_rmask", [128, 512], F32)
            cmask = self.sb(st, "hc_cmask", [64, 512], F32)
            ngb = self.sb(st, "hc_ng", [64, 8, 128], F32)
            b_c = Buf(multi=True)
            S.dma("sp", rmask[:, :], self.c_rmask[:, :], writes=[b_c])
            S.dma("sp", cmask[:, :], self.c_cmask[:, :], writes=[b_c])
            S.dma("sp", ngb[:, :, :], bass.AP(norm_g.tensor, norm_g.offset, [[0, 64], [0, 8], [1, 128]]), writes=[b_c])
            qs = self.sb(st, "hc_qs", [128, 512], F32); b_qs = Buf()
            lf = self.sb(st, "hc_lf", [128, 512], F32); b_lf = Buf()
            kn = self.sb(st, "hc_kn", [128, 512], F32); b_kn = Buf()
            bc = self.sb(st, "hc_bc", [128, 512], F32); b_bc = Buf()
            eb = self.sb(st, "hc_eb", [128, 512], F32); b_eb = Buf()
            enb = self.sb(st, "hc_enb", [128, 512], F32); b_enb = Buf()
            dl = self.sb(st, "hc_dl", [128, 8], F32); b_dl = Buf()
            qe = self.sb(st, "hc_qe", [128, 512], BF16); b_qe = Buf()
            ke = self.sb(st, "hc_ke", [128, 512], BF16); b_ke = Buf()
            keT = self.sb(st, "hc_keT", [64, 8, 128], BF16); b_keT = Buf()
            at = self.sb(st, "hc_at", [64, 512], BF16); b_at = Buf()
            v = self.sb(st, "hc_v", [64, 8, 128], BF16); b_v = Buf()
            sgt = self.sb(st, "hc_sg", [64, 8, 128], F32); b_sgt = Buf()
            gate = self.sb(st, "hc_gate", [64, 8, 128], F32); b_gate = Buf()
            S32 = self.sb(st, "hc_S32", [128, 128], F32); b_S32 = Buf()
            Stmp = self.sb(st, "hc_Stmp", [128, 128], F32); b_Stmp = Buf()
            Sbf = self.sb(st, "hc_Sbf", [128, 128], BF16); b_Sbf = Buf()
            osb = self.sb(st, "hc_osb", [64, 8, 128], F32); b_osb = Buf()
            sq = self.sb(st, "hc_sq", [64, 8, 128], F32); b_sq = Buf()
            ss = self.sb(st, "hc_ss", [64, 8], F32); b_ss = Buf()
            ob = self.sb(st, "hc_ob", [64, 8, 128], BF16); b_ob = Buf()
            oTs = self.sb(st, "hc_oTs", [128, 512], BF16); b_oTs = Buf()
            pS = [self.psg[3], self.ps(st, "hc_pS1", [128, 512], F32)]
            b_pS = [self.b_psg[3], Buf()]
            pa, b_pa = self.psg[0], self.b_psg[0]
            po = [self.psg[1], self.psg[2]]
            b_po = [self.b_psg[1], self.b_psg[2]]
            pk, b_pk = self.pst[0], self.b_pst[0]
            pt, b_pt = self.pst[1], self.b_pst[1]
            obuf = self.bufs["oT"]
            si = 0
            for h in range(32):
                r0 = h * 128
                S.op("dve", lambda e: e.memset(S32[:, :], 0.0), writes=[b_S32])
                S.op("pool", lambda e: e.memset(Sbf[:, :], 0.0), writes=[b_Sbf])
                for blk in range(NB):
                    t0 = blk * 512
                    S.dma("sp", qs[:, :], self.qsT[r0:r0 + 128, t0:t0 + 512], reads=[self.bufs["qsT"]], writes=[b_qs])
                    S.dma("act", lf[:, :], self.logfT[r0:r0 + 128, t0:t0 + 512], reads=[self.bufs["logfT"]], writes=[b_lf])
                    S.dma("sp", kn[:, :], self.kinT[r0:r0 + 128, t0:t0 + 512], reads=[self.bufs["kinT"]], writes=[b_kn])
                    S.dma("act", v[:, :, :], self.v_tok[t0:t0 + 512, r0:r0 + 128].rearrange("(c s) v -> s c v", s=64), reads=[self.bufs["v_tok"]], writes=[b_v])
                    S.dma("sp", sgt[:, :, :], self.sg_tok[t0:t0 + 512, r0:r0 + 128].rearrange("(c s) v -> s c v", s=64), reads=[self.bufs["sg_tok"]], writes=[b_sgt])
                    S.op("dve", lambda e: e.tensor_tensor_scan(bc[:, :], rmask[:, :], lf[:, :], 0.0, ALU.mult, ALU.add), reads=[b_c, b_lf], writes=[b_bc])
                    S.op("act", lambda e: e.activation(eb[:, :], bc[:, :], AF.Exp), reads=[b_bc], writes=[b_eb])
                    S.op("act", lambda e: e.activation(enb[:, :], bc[:, :], AF.Exp, scale=-1.0), reads=[b_bc], writes=[b_enb])
                    S.op("act", lambda e: e.activation(dl[:, :], bc[:, :].rearrange("p (c s) -> p c s", s=64)[:, :, 63], AF.Exp), reads=[b_bc], writes=[b_dl])
                    S.op("dve", lambda e: e.tensor_tensor(qe[:, :], qs[:, :], eb[:, :], ALU.mult), reads=[b_qs, b_eb], writes=[b_qe])
                    S.op("pool", lambda e: e.tensor_tensor(ke[:, :], kn[:, :], enb[:, :], ALU.mult), reads=[b_kn, b_enb], writes=[b_ke])
                    S.op("pool", lambda e: e.tensor_tensor(gate[:, :, :], sgt[:, :, :], ngb[:, :, :], ALU.mult), reads=[b_sgt, b_c], writes=[b_gate])
                    for c in range(8):
                        S.op("pe", lambda e: e.transpose(pk[0:64, c * 128:(c + 1) * 128], ke[:, c * 64:(c + 1) * 64], self.ident[:, :]),
                             reads=[b_ke, self.b_const], writes=[b_pk])
                    S.op("act", lambda e: e.copy(keT[:, :, :], pk[0:64, :].rearrange("p (c k) -> p c k", k=128)), reads=[b_pk], writes=[b_keT])
                    for c in range(8):
                        S.op("pe", lambda e: e.matmul(pa[0:64, c * 64:(c + 1) * 64], ke[:, c * 64:(c + 1) * 64], qe[:, c * 64:(c + 1) * 64], start=True, stop=True),
                             reads=[b_ke, b_qe], writes=[b_pa])
                    S.op("dve", lambda e: e.tensor_tensor(at[:, :], pa[0:64, :], cmask[:, :], ALU.mult), reads=[b_pa, b_c], writes=[b_at])
                    for c in range(8):
                        pq = po[c // 4]
                        bpq = b_po[c // 4]
                        oc = pq[0:64, (c % 4) * 128:(c % 4 + 1) * 128]
                        S.op("pe", lambda e: e.matmul(oc, at[:, c * 64:(c + 1) * 64], v[:, c, :], start=True, stop=False), reads=[b_at, b_v], writes=[bpq])
                        S.op("pe", lambda e: e.matmul(oc, qe[:, c * 64:(c + 1) * 64], Sbf[:, :], start=False, stop=True), reads=[b_qe, b_Sbf], writes=[bpq])
                        k = si % 2
                        si += 1
                        S.op("pe", lambda e: e.matmul(pS[k][:, 0:128], keT[:, c, :], v[:, c, :], start=True, stop=True), reads=[b_keT, b_v], writes=[b_pS[k]])
                        S.op("dve", lambda e: e.tensor_tensor(Stmp[:, :], pS[k][:, 0:128], S32[:, :], ALU.add), reads=[b_pS[k], b_S32], writes=[b_Stmp])
                        S.op("dve", lambda e: e.tensor_scalar(S32[:, :], Stmp[:, :], dl[:, c:c + 1], None, op0=ALU.mult), reads=[b_Stmp, b_dl], writes=[b_S32])
                        S.op("act", lambda e: e.copy(Sbf[:, :], S32[:, :]), reads=[b_S32], writes=[b_Sbf])
                    for k2 in range(2):
                        S.op("act", lambda e: e.copy(osb[:, k2 * 4:(k2 + 1) * 4, :], po[k2][0:64, :].rearrange("p (c v) -> p c v", v=128)), reads=[b_po[k2]], writes=[b_osb])
                    S.op("pool", lambda e: e.tensor_tensor(sq[:, :, :], osb[:, :, :], osb[:, :, :], ALU.mult), reads=[b_osb], writes=[b_sq])
                    S.op("dve", lambda e: e.tensor_reduce(ss[:, :], sq[:, :, :], AX.X, ALU.add), reads=[b_sq], writes=[b_ss])
                    S.op("dve", lambda e: e.tensor_scalar(ss[:, :], ss[:, :], 1.0 / 128, RMS_EPS, op0=ALU.mult, op1=ALU.add), reads=[b_ss], writes=[b_ss])
                    S.op("act", lambda e: e.sqrt(ss[:, :], ss[:, :]), reads=[b_ss], writes=[b_ss])
                    S.op("dve", lambda e: e.reciprocal(ss[:, :], ss[:, :]), reads=[b_ss], writes=[b_ss])
                    for c in range(8):
                        S.op("dve", lambda e: e.scalar_tensor_tensor(ob[:, c, :], osb[:, c, :], ss[:, c:c + 1], gate[:, c, :], op0=ALU.mult, op1=ALU.mult),
                             reads=[b_osb, b_ss, b_gate], writes=[b_ob])
                    for c in range(8):
                        S.op("pe", lambda e: e.transpose(pt[:, c * 64:(c + 1) * 64], ob[:, c, :], self.ident[0:64, 0:64]), reads=[b_ob, self.b_const], writes=[b_pt])
                    S.op("act", lambda e: e.copy(oTs[:, :], pt[:, 0:512]), reads=[b_pt], writes=[b_oTs])
                    S.dma("pool", self.oT[r0:r0 + 128, t0:t0 + 512], oTs[:, :], reads=[b_oTs], writes=[obuf])
        S.barrier()


    def stage_attn(self, layer, wname, kln_g, kln_b):
        S, T, nc = self.S, self.T, self.nc
        NT = T // 128
        TBF = min(1024, T)
        SCL = 1.0 / (math.sqrt(32.0) * 8.0)
        if self.attn_upto < 1:
            return
        with ExitStack() as st:
            tabs = [self.sb(st, "a1_tab%d" % i, [128, TBF], F32) for i in range(4)]
            b_tab = Buf(multi=True)
            pm = [self.sb(st, "a1_pm%d" % i, [128, 128], BF16) for i in range(2)]
            S.dma("sp", pm[0][:, :], self.c_pm128[:, :], writes=[self.b_const])
            S.dma("sp", pm[1][:, :], self.c_pm64[:, :], writes=[self.b_const])
            zb = [self.sb(st, "a1_zb%d" % i, [128, 512], BF16) for i in range(2)]
            b_zb = [Buf() for _ in range(2)]
            t1 = [self.sb(st, "a1_t1%d" % i, [128, 512], F32) for i in range(2)]
            b_t1 = [Buf() for _ in range(2)]
            t2 = [self.sb(st, "a1_t2%d" % i, [128, 512], F32) for i in range(2)]
            b_t2 = [Buf() for _ in range(2)]
            ob = [self.sb(st, "a1_ob%d" % i, [128, 512], BF16) for i in range(2)]
            b_ob = [Buf() for _ in range(2)]
            px = [self.ps(st, "a1_px%d" % i, [128, 512], F32) for i in range(1)] * 2
            b_px = [Buf()] * 2
            cnt = [0]
            cur_tb = [-1]
            ctabs = [self.c_cos128, self.c_sin128, self.c_cos64, self.c_sin64]

            def epi(t0, pn, j, m, pst, psb):
                tb = t0 // TBF
                if tb != cur_tb[0]:
                    cur_tb[0] = tb
                    for k in range(4):
                        S.dma("sp", tabs[k][:, :], ctabs[k][:, tb * TBF:(tb + 1) * TBF], writes=[b_tab])
                i = cnt[0] % 2
                cnt[0] += 1
                off = t0 - tb * TBF
                isq = pn < 12
                ct, sn = (tabs[0], tabs[1]) if isq else (tabs[2], tabs[3])
                S.op("act", lambda e: e.copy(zb[i][:, :], pst[:, :]), reads=[psb], writes=[b_zb[i]])
                S.op("pe", lambda e: e.matmul(px[i][:, :], pm[0 if isq else 1][:, :], zb[i][:, :], start=True, stop=True),
                     reads=[b_zb[i], self.b_const], writes=[b_px[i]])
                S.op("dve", lambda e: e.tensor_tensor(t1[i][:, :], pst[:, :], ct[:, off:off + 512], ALU.mult), reads=[psb, b_tab, b_zb[i]], writes=[b_t1[i]])
                S.op("dve", lambda e: e.tensor_tensor(t2[i][:, :], px[i][:, :], sn[:, off:off + 512], ALU.mult), reads=[b_px[i], b_tab], writes=[b_t2[i]])
                S.op("pool", lambda e: e.tensor_tensor(ob[i][:, :], t1[i][:, :], t2[i][:, :], ALU.add), reads=[b_t1[i], b_t2[i]], writes=[b_ob[i]])
                if pn < 8:
                    dst, bname, n0 = self.qT, "qT", pn * 512 + j * 128
                elif pn < 10:
                    dst, bname, n0 = self.kT, "kT", (pn - 8) * 512 + j * 128
                else:
                    dst, bname, n0 = self.qiT, "qiT", (pn - 12) * 512 + j * 128
                S.dma("pool", dst[n0:n0 + 128, t0:t0 + 512], ob[i][:, :], reads=[b_ob[i]], writes=[self.bufs[bname]])

            self.gemm_F(st, self.xT, self.bufs["xT"], wname, list(range(10)) + [12, 13, 14, 15], epi, "a1")
        S.barrier()
        if self.attn_upto < 2:
            return
        with ExitStack() as st:
            tv = [self.sb(st, "a2_v%d" % i, [128, 512], BF16) for i in range(2)]
            bv = [Buf() for _ in range(2)]
            ones = self.sb(st, "a2_ones", [128, 8], BF16)
            b_ones = Buf()
            S.op("dve", lambda e: e.memset(ones[:, :], 1.0), writes=[b_ones])
            gk = self.sb(st, "a2_gk", [128, 64], F32)
            bk = self.sb(st, "a2_bk", [128, 64], F32)
            b_gk = Buf(multi=True)
            S.dma("sp", gk[:, :], bcast_rows(kln_g, 64), writes=[b_gk])
            S.dma("sp", bk[:, :], bcast_rows(kln_b, 64), writes=[b_gk])
            wa = self.sb(st, "a2_wa", [128, 32], F32); b_wa = Buf()
            ws = self.sb(st, "a2_ws", [128, 32], F32); b_ws = Buf()
            ki = self.sb(st, "a2_ki", [128, 64], F32); b_ki = Buf()
            st6 = self.sb(st, "a2_st6", [128, 6], F32)
            mv = self.sb(st, "a2_mv", [128, 4], F32); b_mv = Buf()
            ck = self.sb(st, "a2_ck", [128, 32], F32)
            sk = self.sb(st, "a2_sk", [128, 32], F32); b_cs = Buf(multi=True)
            ra = self.sb(st, "a2_ra", [128, 64], F32)
            rb = self.sb(st, "a2_rb", [128, 64], F32); b_rab = Buf()
            kib = self.sb(st, "a2_kib", [128, 128], BF16); b_kib = Buf()
            kts = self.sb(st, "a2_kts", [128, 128], BF16); b_kts = Buf()
            cnt = [0]

            def epi(t0, ts, pn, w, pst, psb):
                if pn < 16:
                    i = cnt[0] % 2
                    cnt[0] += 1
                    g0 = (pn - 10) * 4
                    S.op("act", lambda e: e.copy(tv[i][:, :], pst[:, :]), reads=[psb], writes=[bv[i]])
                    S.dma("pool", self.vext[t0:t0 + 128, g0:g0 + 4, 0:128], tv[i][:, :].rearrange("p (g d) -> p g d", d=128), reads=[bv[i]], writes=[self.bufs["vext"]])
                    if pn == 10:
                        S.dma("pool", self.vext[t0:t0 + 128, :, 128:129], ones[:, :].unsqueeze(2), reads=[b_ones], writes=[self.bufs["vext"]])
                    return
                S.op("act", lambda e: e.activation(wa[:, :], pst[:, 0:32], AF.Abs, scale=SCL), reads=[psb], writes=[b_wa])
                S.op("act", lambda e: e.activation(ws[:, :], pst[:, 0:32], AF.Sign), reads=[psb], writes=[b_ws])
                S.dma("sp", self.wabs[t0:t0 + 128, :], wa[:, :], reads=[b_wa], writes=[self.bufs["wabs"]])
                S.dma("sp", self.wsg[t0:t0 + 128, :], ws[:, :], reads=[b_ws], writes=[self.bufs["wsg"]])
                S.op("act", lambda e: e.copy(ki[:, :], pst[:, 32:96]), reads=[psb], writes=[b_ki])
                S.op("dve", lambda e: e.bn_stats(st6[:, :], ki[:, :]), reads=[b_ki], writes=[b_mv])
                S.op("dve", lambda e: e.bn_aggr(mv[:, 0:2], st6[:, :]), reads=[b_mv], writes=[b_mv])
                S.op("dve", lambda e: e.tensor_scalar_add(mv[:, 2:3], mv[:, 1:2], LN_EPS), reads=[b_mv], writes=[b_mv])
                S.op("act", lambda e: e.sqrt(mv[:, 2:3], mv[:, 2:3]), reads=[b_mv], writes=[b_mv])
                S.op("dve", lambda e: e.reciprocal(mv[:, 2:3], mv[:, 2:3]), reads=[b_mv], writes=[b_mv])
                S.op("dve", lambda e: e.scalar_tensor_tensor(mv[:, 3:4], mv[:, 0:1], -1.0, mv[:, 2:3], op0=ALU.mult, op1=ALU.mult), reads=[b_mv], writes=[b_mv])
                S.op("act", lambda e: e.activation(ki[:, :], ki[:, :], AF.Identity, bias=mv[:, 3:4], scale=mv[:, 2:3]), reads=[b_ki, b_mv], writes=[b_ki])
                S.op("dve", lambda e: e.tensor_tensor(ki[:, :], ki[:, :], gk[:, :], ALU.mult), reads=[b_ki, b_gk], writes=[b_ki])
                S.op("dve", lambda e: e.tensor_tensor(ki[:, :], ki[:, :], bk[:, :], ALU.add), reads=[b_ki, b_gk], writes=[b_ki])
                S.dma("sp", ck[:, :], self.c_cosk[t0:t0 + 128, :], writes=[b_cs])
                S.dma("sp", sk[:, :], self.c_sink[t0:t0 + 128, :], writes=[b_cs])
                S.op("dve", lambda e: e.tensor_tensor(ra[:, 0:32], ki[:, 0:32], ck[:, :], ALU.mult), reads=[b_ki, b_cs], writes=[b_rab])
                S.op("dve", lambda e: e.tensor_tensor(ra[:, 32:64], ki[:, 32:64], ck[:, :], ALU.mult), reads=[b_ki, b_cs], writes=[b_rab])
                S.op("dve", lambda e: e.tensor_tensor(rb[:, 0:32], ki[:, 32:64], sk[:, :], ALU.mult), reads=[b_ki, b_cs], writes=[b_rab])
                S.op("dve", lambda e: e.tensor_tensor(rb[:, 32:64], ki[:, 0:32], sk[:, :], ALU.mult), reads=[b_ki, b_cs], writes=[b_rab])
                S.op("dve", lambda e: e.tensor_tensor(kib[:, 0:32], ra[:, 0:32], rb[:, 0:32], ALU.subtract), reads=[b_rab], writes=[b_kib])
                S.op("dve", lambda e: e.tensor_tensor(kib[:, 32:64], ra[:, 32:64], rb[:, 32:64], ALU.add), reads=[b_rab], writes=[b_kib])
                S.op("dve", lambda e: e.tensor_copy(kib[:, 64:128], kib[:, 0:64]), reads=[b_kib], writes=[b_kib])
                q = self.pt_i % 2
                self.pt_i += 1
                pt, b_pt = self.pst[q], self.b_pst[q]
                S.op("pe", lambda e: e.transpose(pt[:, 0:128], kib[:, :], self.ident[:, :]), reads=[b_kib, self.b_const], writes=[b_pt])
                S.op("act", lambda e: e.copy(kts[:, :], pt[:, 0:128]), reads=[b_pt], writes=[b_kts])
                S.dma("sp", self.kiT2[:, t0:t0 + 128], kts[:, :], reads=[b_kts], writes=[self.bufs["kiT2"]])

            self.gemm_T(st, self.xT, self.bufs["xT"], KC, wname, [10, 11, 16], epi, None, "a2")
        S.barrier()
        if self.attn_upto < 3:
            return
        with ExitStack() as st:
            kiT = self.sb(st, "a3_kiT", [128, T], BF16); b_kiT = Buf()
            S.dma("sp", kiT[:, :], self.kiT2[:, :], reads=[self.bufs["kiT2"]], writes=[b_kiT])
            qi = [self.sb(st, "a3_qi%d" % i, [128, 16, 128], BF16) for i in range(2)]
            b_qi = [Buf() for _ in range(2)]
            qz = [[self.sb(st, "a3_qz%d_%d" % (i, par), [128, 16, 128], BF16) for par in range(2)] for i in range(2)]
            b_qz = [Buf(multi=True) for _ in range(2)]
            hm = self.sb(st, "a3_hm", [128, 2], F32)
            S.dma("sp", hm[:, :], self.c_hmask[:, :], writes=[self.b_const])
            wa = [self.sb(st, "a3_wa%d" % i, [128, 32], F32) for i in range(2)]
            ws = [self.sb(st, "a3_ws%d" % i, [128, 32], F32) for i in range(2)]
            b_w = [Buf(multi=True) for _ in range(2)]
            score = self.sb(st, "a3_score", [128, T], F32); b_sc = Buf(multi=True)
            work = self.sb(st, "a3_work", [128, T], F32); b_wk = Buf()
            tmp = [self.sb(st, "a3_tmp%d" % i, [128, 512], F32) for i in range(3)]
            b_tmp = [Buf() for _ in range(3)]
            m8 = self.sb(st, "a3_m8", [128, 8], F32); b_m8 = Buf()
            thr = self.sb(st, "a3_thr", [128, 1], F32); b_thr = Buf()
            maskb = self.sb(st, "a3_mask", [128, T], BF16); b_mask = Buf()
            mts = [self.sb(st, "a3_mts%d" % i, [128, 4, 128], BF16) for i in range(2)]
            b_mts = [Buf() for _ in range(2)]
            ti = 0
            mi = 0
            for i in range(NT):
                t0 = i * 128
                L = (i + 1) * 128
                k = i % 2
                S.dma("sp", qi[k][:, :, :], self.qiT[:, t0:t0 + 128].rearrange("(hp p) t -> p hp t", p=128), reads=[self.bufs["qiT"]], writes=[b_qi[k]])
                S.dma("act", wa[k][:, :], self.wabs[t0:t0 + 128, :], reads=[self.bufs["wabs"]], writes=[b_w[k]])
                S.dma("act", ws[k][:, :], self.wsg[t0:t0 + 128, :], reads=[self.bufs["wsg"]], writes=[b_w[k]])
                for par in range(2):
                    S.op("pool" if par else "dve", lambda e: e.tensor_scalar(qz[k][par][:, :, :], qi[k][:, :, :], hm[:, par:par + 1], None, op0=ALU.mult),
                         reads=[b_qi[k], self.b_const], writes=[b_qz[k]])
                for sb_ in range((L + 511) // 512):
                    s0 = sb_ * 512
                    wdt = min(512, L - s0)
                    for h in range(32):
                        hp, base = h // 2, (h % 2) * 64
                        p = self.ps_i % 4
                        self.ps_i += 1
                        pst, psb = self.psg[p], self.b_psg[p]
                        S.op("pe", lambda e: e.matmul(pst[:, 0:wdt], qz[k][h % 2][:, hp, :], kiT[:, s0:s0 + wdt], start=True, stop=True),
                             reads=[b_qz[k], b_kiT], writes=[psb])
                        tt = ti % 3
                        ti += 1
                        S.op("act", lambda e: e.activation(tmp[tt][:, 0:wdt], pst[:, 0:wdt], AF.Relu, scale=wa[k][:, h:h + 1]), reads=[psb, b_w[k]], writes=[b_tmp[tt]])
                        if h == 0:
                            S.op("dve", lambda e: e.tensor_scalar(score[:, s0:s0 + wdt], tmp[tt][:, 0:wdt], ws[k][:, 0:1], None, op0=ALU.mult),
                                 reads=[b_tmp[tt], b_w[k]], writes=[b_sc])
                        else:
                            S.op("dve", lambda e: e.scalar_tensor_tensor(score[:, s0:s0 + wdt], tmp[tt][:, 0:wdt], ws[k][:, h:h + 1], score[:, s0:s0 + wdt], op0=ALU.mult, op1=ALU.add),
                                 reads=[b_tmp[tt], b_w[k], b_sc], writes=[b_sc])
                S.op("dve", lambda e: e.memset(score[0:64, t0 + 64:t0 + 128], NEG), reads=[b_sc], writes=[b_sc])
                if L >= 256:
                    for rnd in range(32):
                        src = score if rnd == 0 else work
                        S.op("dve", lambda e: e.max(m8[:, :], src[:, 0:L]), reads=[b_sc if rnd == 0 else b_wk], writes=[b_m8])
                        if rnd < 31:
                            S.op("dve", lambda e: e.match_replace(work[:, 0:L], m8[:, :], src[:, 0:L], NEG), reads=[b_m8, b_sc if rnd == 0 else b_wk], writes=[b_wk])
                    S.op("dve", lambda e: e.tensor_scalar_max(thr[:, :], m8[:, 7:8], NEG / 2), reads=[b_m8], writes=[b_thr])
                else:
                    S.op("dve", lambda e: e.memset(thr[:, :], NEG / 2), writes=[b_thr])
                S.op("dve", lambda e: e.tensor_scalar(maskb[:, 0:L], score[:, 0:L], thr[:, 0:1], None, op0=ALU.is_ge), reads=[b_sc, b_thr], writes=[b_mask])
                for j0 in range(0, i + 1, 4):
                    nj = min(4, i + 1 - j0)
                    q = self.pt_i % 2
                    self.pt_i += 1
                    pt, b_pt = self.pst[q], self.b_pst[q]
                    for jj in range(nj):
                        S.op("pe", lambda e: e.transpose(pt[:, jj * 128:(jj + 1) * 128], maskb[:, (j0 + jj) * 128:(j0 + jj + 1) * 128], self.ident[:, :]),
                             reads=[b_mask, self.b_const], writes=[b_pt])
                    mm = mi % 2
                    mi += 1
                    S.op("act", lambda e: e.copy(mts[mm][:, 0:nj, :], pt[:, 0:nj * 128].rearrange("p (j t) -> p j t", t=128)), reads=[b_pt], writes=[b_mts[mm]])
                    S.dma("pool", self.maskT[i, j0:j0 + nj].rearrange("j s t -> s j t"), mts[mm][:, 0:nj, :], reads=[b_mts[mm]], writes=[self.bufs["maskT"]])
        S.barrier()
        if self.attn_upto < 4:
            return
        with ExitStack() as st:
            kTg = [self.sb(st, "a4_k%d" % i, [128, T], BF16) for i in range(2)]
            b_kTg = [Buf() for _ in range(2)]
            vg = [self.sb(st, "a4_v%d" % i, [128, NT, 129], BF16) for i in range(2)]
            b_vg = [Buf() for _ in range(2)]
            qt = [self.sb(st, "a4_q%d" % i, [128, 4, 128], BF16) for i in range(2)]
            b_qt = [Buf() for _ in range(2)]
            mrow = [self.sb(st, "a4_m%d" % i, [128, NT, 128], BF16) for i in range(2)]
            b_mrow = [Buf() for _ in range(2)]
            E = [self.sb(st, "a4_E%d" % i, [128, 4, 128], BF16) for i in range(2)]
            b_E = [Buf() for _ in range(2)]
            Pm = [self.sb(st, "a4_P%d" % i, [128, 4, 128], BF16) for i in range(2)]
            b_Pm = [Buf() for _ in range(2)]
            rec = self.sb(st, "a4_rec", [128, 4], F32); b_rec = Buf()
            ot = self.sb(st, "a4_ot", [128, 4, 128], BF16); b_ot = Buf()
            ots = self.sb(st, "a4_ots", [128, 4, 128], BF16); b_ots = Buf()
            acc = [self.psg[1], self.psg[2], self.psg[3], self.ps(st, "a4_acc3", [128, 512], F32)]
            b_acc = [self.b_psg[1], self.b_psg[2], self.b_psg[3], Buf()]
            ei = 0
            qi_ = 0
            for g in range(8):
                kk = g % 2
                S.dma("sp", kTg[kk][:, :], self.kT[g * 128:(g + 1) * 128, :], reads=[self.bufs["kT"]], writes=[b_kTg[kk]])
                S.dma("act", vg[kk][:, :, :], self.vext[:, g, :].rearrange("(n s) e -> s n e", s=128), reads=[self.bufs["vext"]], writes=[b_vg[kk]])
                for i in range(NT):
                    t0 = i * 128
                    qq = qi_ % 2
                    qi_ += 1
                    S.dma("sp", qt[qq][:, :, :], self.qT[g * 512:(g + 1) * 512, t0:t0 + 128].rearrange("(r d) t -> d r t", d=128), reads=[self.bufs["qT"]], writes=[b_qt[qq]])
                    S.dma("act", mrow[qq][:, 0:i + 1, :], self.maskT[i, 0:i + 1].rearrange("j s t -> s j t"), reads=[self.bufs["maskT"]], writes=[b_mrow[qq]])
                    for j in range(i + 1):
                        p = ei % 2
                        ei += 1
                        pst, psb = self.psg[0], self.b_psg[0]
                        S.op("pe", lambda e: e.matmul(pst[:, :], kTg[kk][:, j * 128:(j + 1) * 128], qt[qq][:, :, :].rearrange("d r t -> d (r t)"), start=True, stop=True),
                             reads=[b_kTg[kk], b_qt[qq]], writes=[psb])
                        S.op("act", lambda e: e.activation(E[p][:, :, :].rearrange("s r t -> s (r t)"), pst[:, :], AF.Exp, scale=128.0 ** -0.5), reads=[psb], writes=[b_E[p]])
                        S.op("dve", lambda e: e.tensor_tensor(Pm[p][:, :, :], E[p][:, :, :], mrow[qq][:, j, :].unsqueeze(1).to_broadcast([128, 4, 128]), ALU.mult),
                             reads=[b_E[p], b_mrow[qq]], writes=[b_Pm[p]])
                        for r in range(4):
                            S.op("pe", lambda e: e.matmul(acc[r][:, 0:129], Pm[p][:, r, :], vg[kk][:, j, :], start=(j == 0), stop=(j == i)),
                                 reads=[b_Pm[p], b_vg[kk]], writes=[b_acc[r]])
                    for r in range(4):
                        a = acc[r]
                        c0 = 0
                        S.op("dve", lambda e: e.reciprocal(rec[:, r:r + 1], a[:, c0 + 128:c0 + 129]), reads=[b_acc[r]], writes=[b_rec])
                        S.op("dve", lambda e: e.tensor_scalar(ot[:, r, :], a[:, c0:c0 + 128], rec[:, r:r + 1], None, op0=ALU.mult), reads=[b_acc[r], b_rec], writes=[b_ot])
                    q = self.pt_i % 2
                    self.pt_i += 1
                    pt, b_pt = self.pst[q], self.b_pst[q]
                    for r in range(4):
                        S.op("pe", lambda e: e.transpose(pt[:, r * 128:(r + 1) * 128], ot[:, r, :], self.ident[:, :]), reads=[b_ot, self.b_const], writes=[b_pt])
                    S.op("act", lambda e: e.copy(ots[:, :, :], pt[:, 0:512].rearrange("p (r t) -> p r t", t=128)), reads=[b_pt], writes=[b_ots])
                    S.dma("pool", self.oT[g * 512:(g + 1) * 512, t0:t0 + 128].rearrange("(r d) t -> d r t", d=128), ots[:, :, :], reads=[b_ots], writes=[self.bufs["oT"]])
        S.barrier()

    def build(self):
        with self.nc.allow_non_contiguous_dma(reason="small strided loads"):
            return self._build()

    def _build(self):
        nc, S, T = self.nc, self.S, self.T
        small = self.small
        NA = 1 if small else 2
        ND = 1 if small else DEPTH
        wi = (lambda j: 0) if small else (lambda j: j)
        layers = self.layers
        x_in = self.din("x", [T, D])
        attn_w_in = self.din("attn_w_in", [NA, D, ATTN_IN])
        attn_w_out = self.din("attn_w_out", [NA, D, D])
        idx_k_ln_g = self.din("idx_k_ln_g", [NA, 64])
        idx_k_ln_b = self.din("idx_k_ln_b", [NA, 64])
        hgrn_w_in = self.din("hgrn_w_in", [NA, D, 4 * D])
        hgrn_lower_bounds = self.din("hgrn_lower_bounds", [DEPTH, D])
        hgrn_norm_g = self.din("hgrn_norm_g", [NA, 128])
        hgrn_w_out = self.din("hgrn_w_out", [NA, D, D])
        mix_ln_g = self.din("mix_ln_g", [ND, D])
        mix_ln_b = self.din("mix_ln_b", [ND, D])
        mlp_w_up = self.din("mlp_w_up", [ND, D, DFF])
        mlp_w_down = self.din("mlp_w_down", [ND, DFF, D])
        mlp_ln_g = self.din("mlp_ln_g", [ND, D])
        mlp_ln_b = self.din("mlp_ln_b", [ND, D])
        self.c_ident = self.din("c_ident", [128, 128], BF16)
        self.c_rmask = self.din("c_rmask", [128, 512])
        self.c_cmask = self.din("c_cmask", [64, 512])
        y_out = self.dout("y", [T, D])
        self.b_y = Buf(multi=True)
        self.x_tok = self.dscr("x_tok", [T, D], F32)
        self.xT = self.dscr("xT", [D, T], BF16)
        self.hT = self.dscr("hT", [DFF, T], BF16)
        self.oT = self.dscr("oT", [D, T], BF16)
        self.qsT = self.dscr("qsT", [D, T], F32)
        self.logfT = self.dscr("logfT", [D, T], F32)
        self.kinT = self.dscr("kinT", [D, T], F32)
        self.v_tok = self.dscr("v_tok", [T, D], BF16)
        self.sg_tok = self.dscr("sg_tok", [T, D], F32)
        NT = T // 128
        self.qT = self.dscr("qT", [D, T], BF16)
        self.kT = self.dscr("kT", [1024, T], BF16)
        self.qiT = self.dscr("qiT", [2048, T], BF16)
        self.vext = self.dscr("vext", [T, 8, 129], BF16)
        self.wabs = self.dscr("wabs", [T, 32], F32)
        self.wsg = self.dscr("wsg", [T, 32], F32)
        self.kiT2 = self.dscr("kiT2", [128, T], BF16)
        self.maskT = self.dscr("maskT", [NT, NT, 128, 128], BF16)
        self.c_pm128 = self.din("c_pm128", [128, 128], BF16)
        self.c_pm64 = self.din("c_pm64", [128, 128], BF16)
        self.c_cos128 = self.din("c_cos128", [128, T])
        self.c_sin128 = self.din("c_sin128", [128, T])
        self.c_cos64 = self.din("c_cos64", [128, T])
        self.c_sin64 = self.din("c_sin64", [128, T])
        self.c_hmask = self.din("c_hmask", [128, 2])
        self.c_cosk = self.din("c_cosk", [T, 32])
        self.c_sink = self.din("c_sink", [T, 32])
        self.setup_common()
        self.setup_lb(hgrn_lower_bounds)
        with ExitStack() as st:
            self.cv_f = [self.sb(st, "cvf%d" % i, [128, 8, 512], F32) for i in range(3)]
            self.cv_b = [self.sb(st, "cvb%d" % i, [128, 8, 512], BF16) for i in range(3)]
            self.cv_fb = [Buf() for _ in range(3)]
            self.cv_bb = [Buf() for _ in range(3)]
            self.cv_i = 0
            for l in layers:
                j = wi(l // 2)
                if l % 2 == 0:
                    if self.do_mixer:
                        self.convert_weight("ain%d" % l, attn_w_in[j], D, 0, ATTN_IN)
                        self.convert_weight("aout%d" % l, attn_w_out[j], D, 0, D)
                else:
                    if self.do_mixer:
                        self.convert_weight("hin%d" % l, hgrn_w_in[j], D, 0, 4 * D)
                        self.convert_weight("hout%d" % l, hgrn_w_out[j], D, 0, D)
                if self.do_mlp:
                    self.convert_weight("up%d" % l, mlp_w_up[wi(l)], D, 0, DFF)
                    self.convert_weight("down%d" % l, mlp_w_down[wi(l)], DFF, 0, D)
        S.barrier()
        self.stage_load_x(x_in)
        for li, l in enumerate(layers):
            j = wi(l // 2)
            lastl = (li == len(layers) - 1)
            if self.do_mixer:
                lastm = lastl and not self.do_mlp
                if l % 2 == 0:
                    self.stage_attn(l, "ain%d" % l, idx_k_ln_g[j], idx_k_ln_b[j])
                    oname = "aout%d" % l
                else:
                    self.stage_hgrn_proj(l, "hin%d" % l)
                    self.stage_hgrn_core(hgrn_norm_g[j])
                    oname = "hout%d" % l
                self.stage_out_ln(self.oT, self.bufs["oT"], KC, oname, mix_ln_g[wi(l)], mix_ln_b[wi(l)],
                                  y_out if lastm else self.x_tok, self.b_y if lastm else self.bufs["x_tok"], "mo%d" % l, last=lastm)
            if self.do_mlp:
                self.stage_mlp_up("up%d" % l, "up%d" % l)
                self.stage_out_ln(self.hT, self.bufs["hT"], DFF // 128, "down%d" % l, mlp_ln_g[wi(l)], mlp_ln_b[wi(l)],
                                  y_out if lastl else self.x_tok, self.b_y if lastl else self.bufs["x_tok"], "dn%d" % l, last=lastl)
        S.barrier()
        self.root.close()
        return nc


def rope_np(T, dim):
    inv = (1.0 / (np.float32(10000.0) ** (np.arange(0, dim, 2, dtype=np.float32) / np.float32(dim)))).astype(np.float32)
    ang = np.arange(T, dtype=np.float32)[:, None] * inv[None, :]
    return np.cos(ang).astype(np.float32), np.sin(ang).astype(np.float32)


def host_consts(T):
    c128, s128 = rope_np(T, 128)
    c64, s64 = rope_np(T, 64)
    d = np.arange(128)
    cos128 = c128[:, d % 64].T.copy()
    sin128 = (s128[:, d % 64] * np.where(d < 64, -1.0, 1.0)[None, :]).T.astype(np.float32).copy()
    cos64 = c64[:, d % 32].T.copy()
    sin64 = (s64[:, d % 32] * np.where((d % 64) < 32, -1.0, 1.0)[None, :]).T.astype(np.float32).copy()
    pm128 = np.zeros((128, 128), np.float32)
    pm128[d, (d + 64) % 128] = 1.0
    pm64 = np.zeros((128, 128), np.float32)
    pm64[d, (d // 64) * 64 + ((d % 64) + 32) % 64] = 1.0
    rm = np.ones((128, 512), np.float32)
    rm[:, ::64] = 0.0
    s = np.arange(64)[:, None]
    t = np.arange(512)[None, :] % 64
    cm = (s <= t).astype(np.float32)
    bf = ml_dtypes.bfloat16
    return {"c_ident": np.eye(128, dtype=np.float32).astype(bf), "c_rmask": rm, "c_cmask": cm,
            "c_pm128": pm128.astype(bf), "c_pm64": pm64.astype(bf),
            "c_cos128": cos128, "c_sin128": sin128, "c_cos64": cos64, "c_sin64": sin64,
            "c_cosk": c64.copy(), "c_sink": s64.copy(),
            "c_hmask": np.stack([(d < 64), (d >= 64)], 1).astype(np.float32)}


_INPUT_KEYS = ("attn_w_in", "attn_w_out", "idx_k_ln_g", "idx_k_ln_b", "hgrn_w_in", "hgrn_lower_bounds", "hgrn_norm_g",
               "hgrn_w_out", "mix_ln_g", "mix_ln_b", "mlp_w_up", "mlp_w_down", "mlp_ln_g", "mlp_ln_b")


def kernel(**inputs):
    x = np.asarray(inputs["x"], dtype=np.float32)
    B, T, _ = x.shape
    prog = Prog(T)
    nc = prog.build()
    consts = host_consts(T)
    in_maps = []
    for b in range(B):
        m = {"x": np.ascontiguousarray(x[b])}
        for k in _INPUT_KEYS:
            m[k] = np.ascontiguousarray(np.asarray(inputs[k], dtype=np.float32))
        m.update(consts)
        in_maps.append(m)
    res = run_bass_kernel_spmd(nc, in_maps, core_ids=list(range(B)))
    return np.stack([np.asarray(res.results[b]["y"], dtype=np.float32) for b in range(B)], axis=0)
```
